# Optimizing a Trainium2 kernel written in Bass

```python
import math
import jax, jax.numpy as jnp
from jax import lax
import numpy as np

D_MODEL = 1024
BATCH = 8
SEQ = 4096
DEPTH = 2

CTX_LEN = 256
GRID_W = 64
ROPE_THETA = 10000.0
EPS = 1e-6
Q_BLOCK = 128
N_MOD = 9

D_FF = 2816

CHUNK = 128
A_GROUPS = 8
A_WIDTH = 1024
A_GROUP_W = A_WIDTH // A_GROUPS

B_HEADS = 8
B_HEAD_DIM = 64
B_WIDTH = B_HEADS * 2 * B_HEAD_DIM

C_HEADS = 16
C_NOPE = 64
C_ROPE = 32
C_VDIM = 64
C_Q_RANK = 512
C_KV_RANK = 256
C_WIDTH = C_HEADS * C_VDIM

N_BRANCH = 3
BRANCH_W = 1024
IN_SIZES = (2 * A_WIDTH, B_WIDTH, B_WIDTH, B_WIDTH, C_Q_RANK, C_KV_RANK, C_ROPE, N_BRANCH * D_MODEL)
D_IN = 2 * A_WIDTH + 3 * B_WIDTH + C_Q_RANK + C_KV_RANK + C_ROPE + N_BRANCH * D_MODEL

kernel_name = "hybrid_gated_gmlp_diffattn_mla_macaron_dit"


def rms_norm(x, g):
    xf = x.astype(jnp.float32)
    y = xf * lax.rsqrt(jnp.mean(xf * xf, axis=-1, keepdims=True) + EPS)
    return (y * g.astype(jnp.float32)).astype(x.dtype)


def layer_norm(x, g, b):
    xf = x.astype(jnp.float32)
    mu = jnp.mean(xf, axis=-1, keepdims=True)
    xc = xf - mu
    y = xc * lax.rsqrt(jnp.mean(xc * xc, axis=-1, keepdims=True) + EPS)
    return (y * g.astype(jnp.float32) + b.astype(jnp.float32)).astype(x.dtype)


def modulate(h, shift, scale):
    return h * (1.0 + scale) + shift


def swiglu(h, w13, w2):
    a, b = jnp.split(h @ w13, 2, axis=-1)
    return (jax.nn.silu(a) * b) @ w2


def apply_rope(x, cos, sin):
    x1, x2 = jnp.split(x, 2, axis=-1)
    return jnp.concatenate([x1 * cos - x2 * sin, x2 * cos + x1 * sin], axis=-1)


def axial_rope_tables(rows, rot_dim, dtype):
    row = jnp.repeat(jnp.arange(rows), GRID_W)
    col = jnp.tile(jnp.arange(GRID_W), rows)
    n_freq = rot_dim // 4
    freqs = ROPE_THETA ** (-jnp.arange(n_freq, dtype=jnp.float32) / n_freq)
    ang = jnp.concatenate([row[:, None] * freqs, col[:, None] * freqs], axis=-1)
    return jnp.cos(ang).astype(dtype), jnp.sin(ang).astype(dtype)


def sweep_query_blocks(fn, q):
    bsz, n = q.shape[0], q.shape[1]
    nb = n // Q_BLOCK
    qb = jnp.moveaxis(q.reshape((bsz, nb, Q_BLOCK) + q.shape[2:]), 1, 0)
    ob = lax.map(fn, qb)
    return jnp.moveaxis(ob, 0, 1).reshape((bsz, n) + ob.shape[3:])


def chunk_spatial_gating(a, ln_g, ln_b, w_s, b_s):
    u, v = jnp.split(jax.nn.gelu(a), 2, axis=-1)
    v = layer_norm(v, ln_g, ln_b)
    bsz, n = v.shape[0], v.shape[1]
    v = v.reshape(bsz, n // CHUNK, CHUNK, A_GROUPS, A_GROUP_W)
    z = jnp.einsum('gij,bnjgc->bnigc', w_s, v) + b_s.T[:, :, None]
    return u * z.reshape(bsz, n, A_WIDTH)


def diff_softmax_attend(q, k, v, lam):
    s = jnp.einsum('bqhmd,bkhmd->bhmqk', q, k).astype(jnp.float32) * (B_HEAD_DIM ** -0.5)
    p = jax.nn.softmax(s, axis=-1)
    w = p[:, :, 0] - lam * p[:, :, 1]
    return jnp.einsum('bhqk,bkhe->bqhe', w.astype(v.dtype), v)


def diff_post(o, subln_g, lam_init):
    bsz, n = o.shape[0], o.shape[1]
    return (rms_norm(o, subln_g) * (1.0 - lam_init)).reshape(bsz, n, B_WIDTH)


def mla_attend(q, kn, kr, v):
    qn, qr = q[..., :C_NOPE], q[..., C_NOPE:]
    s = (jnp.einsum('bqhd,bkhd->bhqk', qn, kn) + jnp.einsum('bqhr,bkr->bhqk', qr, kr)).astype(jnp.float32)
    p = jax.nn.softmax(s * ((C_NOPE + C_ROPE) ** -0.5), axis=-1)
    return jnp.einsum('bhqk,bkhe->bqhe', p.astype(v.dtype), v)


def gated_merge(oa, ob, oc, g, b_gate, w_branch, w_out):
    ga, gb, gc = jnp.split(jax.nn.sigmoid(g + b_gate), N_BRANCH, axis=-1)
    y = ga * (oa @ w_branch[0]) + gb * (ob @ w_branch[1]) + gc * (oc @ w_branch[2])
    return y @ w_out


def token_mix(hx, hc, w_in, b_gate, ln_v_g, ln_v_b, spatial_w, spatial_b, lam, lam_init, subln_g,
              q_norm_g, w_uq, kv_norm_g, w_ukv, w_branch, w_out, cos_b, sin_b, cos_c, sin_c, ctx_out):
    offsets = [int(o) for o in np.cumsum(IN_SIZES)[:-1]]
    a_x, qb_x, kb_x, vb_x, cq_x, ckv_x, kr_x, g_x = jnp.split(hx @ w_in, offsets, axis=-1)
    a_c, qb_c, kb_c, vb_c, cq_c, ckv_c, kr_c, g_c = jnp.split(hc @ w_in, offsets, axis=-1)

    def diff_heads(q, k, v):
        bsz, L = q.shape[0], q.shape[1]
        return (q.reshape(bsz, L, B_HEADS, 2, B_HEAD_DIM),
                k.reshape(bsz, L, B_HEADS, 2, B_HEAD_DIM),
                v.reshape(bsz, L, B_HEADS, 2 * B_HEAD_DIM))

    def mla_heads(cq, ckv):
        bsz, L = cq.shape[0], cq.shape[1]
        q = (rms_norm(cq, q_norm_g) @ w_uq).reshape(bsz, L, C_HEADS, C_NOPE + C_ROPE)
        kv = (rms_norm(ckv, kv_norm_g) @ w_ukv).reshape(bsz, L, C_HEADS, C_NOPE + C_VDIM)
        return q, kv[..., :C_NOPE], kv[..., C_NOPE:]

    bsz, n = hx.shape[0], hx.shape[1]

    qbx, kbx, vbx = diff_heads(qb_x, kb_x, vb_x)
    rb_cos, rb_sin = cos_b[:, None, None, :], sin_b[:, None, None, :]
    qbx = apply_rope(qbx, rb_cos, rb_sin)
    kbx = apply_rope(kbx, rb_cos, rb_sin)
    qbc, kbc, vbc = diff_heads(qb_c, kb_c, vb_c)
    kb_all = jnp.concatenate([kbc, kbx], axis=1)
    vb_all = jnp.concatenate([vbc, vbx], axis=1)
    ob_x = diff_post(sweep_query_blocks(lambda qb: diff_softmax_attend(qb, kb_all, vb_all, lam), qbx),
                     subln_g, lam_init)

    qcx, kncx, vcx = mla_heads(cq_x, ckv_x)
    qcx = jnp.concatenate([qcx[..., :C_NOPE],
                           apply_rope(qcx[..., C_NOPE:], cos_c[:, None, :], sin_c[:, None, :])], axis=-1)
    krx = apply_rope(kr_x, cos_c, sin_c)
    qcc, kncc, vcc = mla_heads(cq_c, ckv_c)
    kn_all = jnp.concatenate([kncc, kncx], axis=1)
    kr_all = jnp.concatenate([kr_c, krx], axis=1)
    vc_all = jnp.concatenate([vcc, vcx], axis=1)
    oc_x = sweep_query_blocks(lambda qb: mla_attend(qb, kn_all, kr_all, vc_all), qcx).reshape(bsz, n, C_WIDTH)

    oa_x = chunk_spatial_gating(a_x, ln_v_g, ln_v_b, spatial_w, spatial_b)

    out_x = gated_merge(oa_x, ob_x, oc_x, g_x, b_gate, w_branch, w_out)
    if not ctx_out:
        return out_x, None

    ob_c = diff_post(diff_softmax_attend(qbc, kbc, vbc, lam), subln_g, lam_init)
    oc_c = mla_attend(qcc, kncc, kr_c, vcc).reshape(bsz, hc.shape[1], C_WIDTH)
    oa_c = chunk_spatial_gating(a_c, ln_v_g, ln_v_b, spatial_w, spatial_b)
    out_c = gated_merge(oa_c, ob_c, oc_c, g_c, b_gate, w_branch, w_out)
    return out_x, out_c


def setup_inputs(seed: int = 0) -> dict:
    key = jax.random.key(seed)
    ks = jax.random.split(key, 40)
    f32 = jnp.float32

    def nrm(k, shape, fan_in, mult=1.0):
        return jax.random.normal(k, shape, f32) * (mult * fan_in ** -0.5)

    def gain(k, shape):
        return 1.0 + 0.05 * jax.random.normal(k, shape, f32)

    def small(k, shape, s=0.02):
        return s * jax.random.normal(k, shape, f32)

    L = DEPTH
    return {
        "x": jax.random.normal(ks[0], (BATCH, SEQ, D_MODEL), f32),
        "c": jax.random.normal(ks[1], (BATCH, D_MODEL), f32),
        "ctx": jax.random.normal(ks[2], (BATCH, CTX_LEN, D_MODEL), f32),
        "c_ctx": jax.random.normal(ks[3], (D_MODEL,), f32),
        "ada_w": nrm(ks[4], (L, D_MODEL, N_MOD * D_MODEL), D_MODEL, 0.5),
        "ada_b": small(ks[5], (L, N_MOD * D_MODEL)),
        "norm_ffn1": gain(ks[6], (L, D_MODEL)),
        "ffn1_w13": nrm(ks[7], (L, D_MODEL, 2 * D_FF), D_MODEL),
        "ffn1_w2": nrm(ks[8], (L, D_FF, D_MODEL), D_FF),
        "norm_mix": gain(ks[9], (L, D_MODEL)),
        "w_in": nrm(ks[10], (L, D_MODEL, D_IN), D_MODEL),
        "b_gate": small(ks[11], (L, N_BRANCH * D_MODEL), 0.1),
        "ln_v_g": gain(ks[12], (L, A_WIDTH)),
        "ln_v_b": small(ks[13], (L, A_WIDTH)),
        "spatial_w": nrm(ks[14], (L, A_GROUPS, CHUNK, CHUNK), CHUNK, 0.5),
        "spatial_b": gain(ks[15], (L, A_GROUPS, CHUNK)),
        "lambda_q1": small(ks[16], (L, B_HEAD_DIM), 0.1),
        "lambda_k1": small(ks[17], (L, B_HEAD_DIM), 0.1),
        "lambda_q2": small(ks[18], (L, B_HEAD_DIM), 0.1),
        "lambda_k2": small(ks[19], (L, B_HEAD_DIM), 0.1),
        "subln_g": gain(ks[20], (L, 2 * B_HEAD_DIM)),
        "q_norm_g": gain(ks[21], (L, C_Q_RANK)),
        "w_uq": nrm(ks[22], (L, C_Q_RANK, C_HEADS * (C_NOPE + C_ROPE)), C_Q_RANK),
        "kv_norm_g": gain(ks[23], (L, C_KV_RANK)),
        "w_ukv": nrm(ks[24], (L, C_KV_RANK, C_HEADS * (C_NOPE + C_VDIM)), C_KV_RANK),
        "w_branch": nrm(ks[25], (L, N_BRANCH, BRANCH_W, D_MODEL), BRANCH_W),
        "w_out": nrm(ks[26], (L, D_MODEL, D_MODEL), D_MODEL),
        "norm_ffn2": gain(ks[27], (L, D_MODEL)),
        "ffn2_w13": nrm(ks[28], (L, D_MODEL, 2 * D_FF), D_MODEL),
        "ffn2_w2": nrm(ks[29], (L, D_FF, D_MODEL), D_FF),
        "norm_final": gain(ks[30], (D_MODEL,)),
    }


def reference(x, c, ctx, c_ctx, ada_w, ada_b, norm_ffn1, ffn1_w13, ffn1_w2, norm_mix, w_in, b_gate,
              ln_v_g, ln_v_b, spatial_w, spatial_b, lambda_q1, lambda_k1, lambda_q2, lambda_k2, subln_g,
              q_norm_g, w_uq, kv_norm_g, w_ukv, w_branch, w_out, norm_ffn2, ffn2_w13, ffn2_w2, norm_final):
    n = x.shape[1]
    ROWS = n // GRID_W
    cos_b, sin_b = axial_rope_tables(ROWS, B_HEAD_DIM, x.dtype)
    cos_c, sin_c = axial_rope_tables(ROWS, C_ROPE, x.dtype)
    s_lat = jax.nn.silu(c)
    s_ctx = jax.nn.silu(c_ctx)

    for l in range(DEPTH):
        ctx_out = l < DEPTH - 1
        m_x = jnp.split((s_lat @ ada_w[l] + ada_b[l])[:, None, :], N_MOD, axis=-1)
        m_c = jnp.split(s_ctx @ ada_w[l] + ada_b[l], N_MOD, axis=-1)

        x = x + 0.5 * m_x[2] * swiglu(modulate(rms_norm(x, norm_ffn1[l]), m_x[0], m_x[1]), ffn1_w13[l], ffn1_w2[l])
        ctx = ctx + 0.5 * m_c[2] * swiglu(modulate(rms_norm(ctx, norm_ffn1[l]), m_c[0], m_c[1]), ffn1_w13[l], ffn1_w2[l])

        hx = modulate(rms_norm(x, norm_mix[l]), m_x[3], m_x[4])
        hc = modulate(rms_norm(ctx, norm_mix[l]), m_c[3], m_c[4])
        lam_init = 0.8 - 0.6 * math.exp(-0.3 * l)
        lam = (jnp.exp(jnp.sum(lambda_q1[l].astype(jnp.float32) * lambda_k1[l].astype(jnp.float32)))
               - jnp.exp(jnp.sum(lambda_q2[l].astype(jnp.float32) * lambda_k2[l].astype(jnp.float32)))
               + lam_init)
        out_x, out_c = token_mix(hx, hc, w_in[l], b_gate[l], ln_v_g[l], ln_v_b[l], spatial_w[l], spatial_b[l],
                                 lam, lam_init, subln_g[l], q_norm_g[l], w_uq[l], kv_norm_g[l], w_ukv[l],
                                 w_branch[l], w_out[l], cos_b, sin_b, cos_c, sin_c, ctx_out)
        x = x + m_x[5] * out_x

        x = x + 0.5 * m_x[8] * swiglu(modulate(rms_norm(x, norm_ffn2[l]), m_x[6], m_x[7]), ffn2_w13[l], ffn2_w2[l])
        if ctx_out:
            ctx = ctx + m_c[5] * out_c
            ctx = ctx + 0.5 * m_c[8] * swiglu(modulate(rms_norm(ctx, norm_ffn2[l]), m_c[6], m_c[7]), ffn2_w13[l], ffn2_w2[l])

    return rms_norm(x, norm_final)
```

```python
import math
from contextlib import ExitStack

import numpy as np
import concourse.bass as bass
import concourse.mybir as mybir
from concourse.bass_utils import run_bass_kernel_spmd

F32 = mybir.dt.float32
BF16 = mybir.dt.bfloat16
AF = mybir.ActivationFunctionType
ALU = mybir.AluOpType
AX = mybir.AxisListType

ENGS = ("pe", "act", "dve", "pool", "sp")
SEM_LIMIT = 20000


class Buf:
    __slots__ = ("name", "w", "r")

    def __init__(self, name):
        self.name = name
        self.w = None
        self.r = {}


class Sched:
    def __init__(self, nc, n_dma_sems=32):
        self.nc = nc
        self.ops = {e: [] for e in ENGS}
        self.n_dma_sems = n_dma_sems
        self.dma_cnt = [0] * n_dma_sems
        self.dma_last = [None] * n_dma_sems
        self.dma_rr_q = {}
        self.all_dma_events = []

    def op(self, eng, fn, reads=(), writes=(), dma=False, extra_waits=()):
        waits = set(extra_waits)
        for b in reads:
            if b.w is not None:
                waits.add(b.w)
        for b in writes:
            if b.w is not None:
                waits.add(b.w)
            waits.update(b.r.values())
        if dma:
            lo, hi = (0, self.n_dma_sems // 2) if eng == "sp" else (self.n_dma_sems // 2, self.n_dma_sems)
            k = self.dma_rr_q.get(eng, lo)
            self.dma_rr_q[eng] = lo + (k + 1 - lo) % (hi - lo)
            if self.dma_last[k] is not None:
                waits.add(self.dma_last[k])
            self.dma_cnt[k] += 1
            ev = ("D", k, 16 * self.dma_cnt[k])
            self.dma_last[k] = ev
        else:
            ev = ("E", eng, len(self.ops[eng]))
        if eng == "pe":
            waits = {w for w in waits if not (w[0] == "E" and w[1] == "pe")}
        self.ops[eng].append((fn, waits, dma, ev))
        for b in reads:
            key = ("D", ev[1]) if dma else ("E", eng)
            b.r[key] = ev
        for b in writes:
            b.w = ev
            b.r = {}
        return ev

    def barrier(self):
        evs = set()
        for e in ENGS:
            n = len(self.ops[e])
            for i in range(n - 1, -1, -1):
                if not self.ops[e][i][2]:
                    evs.add(self.ops[e][i][3])
                    break
        for ev in self.dma_last:
            if ev is not None:
                evs.add(ev)
        for e in ENGS:
            self.op(e, lambda eng: eng.nop(), extra_waits=evs)

    def emit(self, stack):
        nc = self.nc
        needed = {e: set() for e in ENGS}
        for e in ENGS:
            for (_fn, waits, _dma, _ev) in self.ops[e]:
                for w in waits:
                    if w[0] == "E":
                        needed[w[1]].add(w[2])
        cnt_of = {e: {} for e in ENGS}
        n_sems = {}
        for e in ENGS:
            c = 0
            for i in range(len(self.ops[e])):
                if i in needed[e]:
                    c += 1
                    cnt_of[e][i] = c
            n_sems[e] = max(1, (c + SEM_LIMIT - 1) // SEM_LIMIT)
        esems = {e: [stack.enter_context(nc.semaphore(f"s_{e}{j}")) for j in range(n_sems[e])]
                 for e in ENGS}
        dsems = [stack.enter_context(nc.semaphore(f"s_dma{j}")) for j in range(self.n_dma_sems)]
        handles = {"pe": nc.tensor, "act": nc.scalar, "dve": nc.vector, "pool": nc.gpsimd, "sp": nc.sync}

        def emit_engine(e, eng):
            waited_e = {x: 0 for x in ENGS}
            waited_d = [0] * self.n_dma_sems
            for i, (fn, waits, dma, ev) in enumerate(self.ops[e]):
                for w in sorted(waits):
                    if w[0] == "E":
                        c = cnt_of[w[1]][w[2]]
                        if c <= waited_e[w[1]]:
                            continue
                        waited_e[w[1]] = c
                        eng.wait_ge(esems[w[1]][(c - 1) // SEM_LIMIT], (c - 1) % SEM_LIMIT + 1)
                    else:
                        if w[2] <= waited_d[w[1]]:
                            continue
                        waited_d[w[1]] = w[2]
                        eng.wait_ge(dsems[w[1]], w[2])
                ins = fn(eng)
                if dma:
                    ins.then_inc(dsems[ev[1]], 16)
                elif i in cnt_of[e]:
                    c = cnt_of[e][i]
                    ins.then_inc(esems[e][(c - 1) // SEM_LIMIT], 1)

        with nc.Block() as block:
            @block.tensor
            def _(eng):
                emit_engine("pe", eng)

            @block.scalar
            def _(eng):
                emit_engine("act", eng)

            @block.vector
            def _(eng):
                emit_engine("dve", eng)

            @block.gpsimd
            def _(eng):
                emit_engine("pool", eng)

            @block.sync
            def _(eng):
                emit_engine("sp", eng)


D = 1024
SEQ = 4096
CTX = 256
T = CTX + SEQ
NL = 2
DFF = 2816
EPS = 1e-6
UNITS = [(0, CTX)] + [(CTX + 512 * i, 512) for i in range(8)]
SUPER = [[0], [1, 2], [3, 4], [5, 6], [7, 8]]
SUPER_F = [[0, 1, 2], [3, 4], [5, 6], [7, 8]]
SUPER_FL = [[1, 2], [3, 4], [5, 6], [7, 8]]
NKT = T // 128
LAM_INIT = [0.8 - 0.6 * math.exp(-0.3 * l) for l in range(NL)]
C_C, C_CCTX = 0, 8
C_L0 = 16
CL_ADAB, CL_NF1, CL_NMIX, CL_NF2, CL_BG, CL_SUBLN, CL_QNG, CL_KVNG = 0, 72, 80, 88, 96, 120, 121, 125
CL_N = 127
C_NFIN = C_L0 + NL * CL_N
NCOL = C_NFIN + 8
WF_U, WF_QB, WF_QBS, WF_KB, WF_KBS, WF_CQ, WF_CKV, WF_KR, WF_KRS, WF_G = 0, 8, 16, 24, 32, 40, 44, 46, 47, 48
WF_N = 72


class DT:
    def __init__(self, ap, nrow, name):
        self.ap = ap
        self.g = [[Buf(f"{name}_{r}_{u}") for u in range(len(UNITS))] for r in range(nrow)]

    def b(self, rows=None, units=None):
        rows = range(len(self.g)) if rows is None else rows
        units = range(len(UNITS)) if units is None else units
        return [self.g[r][u] for r in rows for u in units]


def build_program(dump=False, stop_after=None):
    nc = bass.Bass("TRN2", target_bir_lowering=False)
    S = Sched(nc)

    def din(name, shape):
        return nc.dram_tensor(name, list(shape), F32, kind="ExternalInput").ap()

    def dscr(name, shape, dt):
        return nc.dram_tensor(name, list(shape), dt, kind="ExternalOutput" if dump else "Internal").ap()

    x0 = DT(din("xT0", [D, T]), 1, "x0")
    cols_d = din("cols", [128, NCOL])
    lamcols_d = din("lamcols", [64, NL * 4])
    adaw_d = din("adaw", [NL, 9, 128, 8 * 1024])
    w13s_d = din("w13s", [NL, 2, 22, 128, 2 * 8 * 128])
    w2s_d = din("w2s", [NL, 2, 8, 128, 22 * 128])
    winf_d = din("winf", [NL, WF_N, 128, 8 * 128])
    wint_d = din("wint", [NL, 4, 128, 8 * 512])
    wuq_d = din("wuq", [NL, 16, 128, 4 * 192])
    wukvk_d = din("wukvk", [NL, 8, 128, 2 * 128])
    wukvv_d = din("wukvv", [NL, 2, 128, 2 * 512])
    wbr_d = din("wbr", [NL, 3, 8, 128, 8 * 128])
    wout_d = din("wout", [NL, 8, 128, 8 * 128])
    wsT_d = din("wsT", [NL, 128, 1024])
    lnv_d = din("lnv", [NL, 2, 128, 1024])
    bsb_d = din("bsb", [NL, 128, 1024])
    ropeB_d = din("ropeB", [2, 128, SEQ])
    ropeC_d = din("ropeC", [2, 32, SEQ])
    out_d = DT(nc.dram_tensor("outT", [D, SEQ], F32, kind="ExternalOutput").ap(), 1, "out")

    res = DT(dscr("res", [D, T], F32), 1, "res")
    oaT = DT(dscr("oaT", [D, T], BF16), 1, "oaT")
    obT = DT(dscr("obT", [D, T], BF16), 8, "obT")
    ocT = DT(dscr("ocT", [D, T], BF16), 16, "ocT")
    qbT = DT(dscr("qbT", [D, T], BF16), 8, "qbT")
    kbT = DT(dscr("kbT", [D, T], BF16), 8, "kbT")
    vbt = DT(dscr("vbt", [T, D], BF16), 1, "vbt")
    qcT = DT(dscr("qcT", [16 * 96, T], BF16), 16, "qcT")
    kncT = DT(dscr("kncT", [16 * 64, T], BF16), 8, "kncT")
    krT = DT(dscr("krT", [32, T], BF16), 1, "krT")
    vct = DT(dscr("vct", [T, D], BF16), 1, "vct")
    gT = DT(dscr("gT", [3 * D, T], BF16), 1, "gT")

    glob = ExitStack()

    tile_ctr = [0]

    def tile(stack, name, shape, dt):
        tile_ctr[0] += 1
        t = stack.enter_context(nc.sbuf_tensor(f"sb{tile_ctr[0]}_{name}", list(shape), dt))
        return t, Buf(name)

    ps_t = glob.enter_context(nc.psum_tensor("ps", [128, 8, 512], F32))
    PB = [Buf(f"psb{k}") for k in range(8)]

    def dma(eng, out, in_, reads, writes):
        S.op(eng, lambda e: e.dma_start(out=out, in_=in_), reads=reads, writes=writes, dma=True)

    def mm(out, lhsT, rhs, start, stop, reads, writes):
        S.op("pe", lambda e: e.matmul(out, lhsT=lhsT, rhs=rhs, start=start, stop=stop),
             reads=reads, writes=writes)

    def act(out, in_, func, reads, writes, bias=None, scale=None, accum_out=None):
        kw = {}
        if bias is not None:
            kw["bias"] = bias
        if scale is not None:
            kw["scale"] = scale
        if accum_out is not None:
            kw["accum_out"] = accum_out
        S.op("act", lambda e: e.activation(out=out, in_=in_, func=func, **kw), reads=reads, writes=writes)

    def tt(out, in0, in1, op, reads, writes, eng="dve"):
        S.op(eng, lambda e: e.tensor_tensor(out=out, in0=in0, in1=in1, op=op), reads=reads, writes=writes)

    def ts(out, in0, s1, s2, op0, op1, reads, writes):
        if op1 is None:
            S.op("dve", lambda e: e.tensor_scalar(out=out, in0=in0, scalar1=s1, scalar2=None, op0=op0),
                 reads=reads, writes=writes)
        else:
            S.op("dve", lambda e: e.tensor_scalar(out=out, in0=in0, scalar1=s1, scalar2=s2, op0=op0, op1=op1),
                 reads=reads, writes=writes)

    def stt(out, in0, scalar, in1, op0, op1, reads, writes):
        S.op("dve", lambda e: e.scalar_tensor_tensor(out=out, in0=in0, scalar=scalar, in1=in1, op0=op0, op1=op1),
             reads=reads, writes=writes)

    def recip(out, in_, reads, writes):
        S.op("dve", lambda e: e.reciprocal(out=out, in_=in_), reads=reads, writes=writes)

    def vcopy(out, in_, reads, writes, eng="dve"):
        S.op(eng, lambda e: e.tensor_copy(out=out, in_=in_), reads=reads, writes=writes)

    def memset(eng, ap, val, writes):
        S.op(eng, lambda e: e.memset(ap, val), writes=writes)

    cols, Bcols = tile(glob, "cols", [128, NCOL], F32)
    lamc, Blamc = tile(glob, "lamc", [64, NL * 4], F32)
    mods, Bmods = tile(glob, "mods", [128, NL, 72, 2], F32)
    drv, Bdrv = tile(glob, "drv", [128, NL, 3, 3, 2, 8], F32)
    ones_bf, Bones_bf = tile(glob, "ones_bf", [128, 128], BF16)
    ones_f, Bones_f = tile(glob, "ones_f", [128, 128], F32)
    neglam, Bneglam = tile(glob, "neglam", [128, NL], F32)
    sublnc, Bsublnc = tile(glob, "sublnc", [128, NL], F32)

    dma("sp", cols[:], cols_d, [], [Bcols])
    dma("sp", lamc[:], lamcols_d, [], [Blamc])
    memset("pool", ones_bf[:], 1.0, [Bones_bf])
    memset("pool", ones_f[:], 1.0, [Bones_f])

    def lcol(l, off, n=1):
        o = C_L0 + l * CL_N + off
        return cols[:, o:o + n]

    with ExitStack() as ph:
        s_col, Bs_col = tile(ph, "s_col", [128, 8, 2], BF16)
        aslab = [tile(ph, f"aslab{i}", [128, 8, 1024], BF16) for i in range(2)]
        prod, Bprod = tile(ph, "prod", [64, 2], F32)
        ee, Bee = tile(ph, "ee", [128, 2], F32)
        act(s_col[:, :, 0], cols[:, C_C:C_C + 8], AF.Silu, [Bcols], [Bs_col])
        act(s_col[:, :, 1], cols[:, C_CCTX:C_CCTX + 8], AF.Silu, [Bcols], [Bs_col])
        it = 0
        for l in range(NL):
            for m in range(9):
                sl, Bsl = aslab[it % 2]
                it += 1
                dma("pool", sl[:].rearrange("p k c -> p (k c)"), adaw_d[l, m], [], [Bsl])
                for fc in range(8):
                    for kc in range(8):
                        mm(ps_t[:, 0, fc * 2:fc * 2 + 2], sl[:, kc, fc * 128:(fc + 1) * 128], s_col[:, kc, :],
                           kc == 0, kc == 7, [Bsl, Bs_col], [PB[0]])
                for tc in range(2):
                    tt(mods[:, l, m * 8:(m + 1) * 8, tc], ps_t[:, 0, tc:16:2], lcol(l, CL_ADAB + m * 8, 8),
                       ALU.add, [PB[0], Bcols], [Bmods])
            for sub, (ish, isc, ig, cg, gmul) in enumerate(((0, 1, 2, CL_NF1, 0.5), (3, 4, 5, CL_NMIX, 1.0),
                                                           (6, 7, 8, CL_NF2, 0.5))):
                for tc in range(2):
                    stt(drv[:, l, sub, 0, tc, :], mods[:, l, isc * 8:(isc + 1) * 8, tc], 1.0, lcol(l, cg, 8),
                        ALU.add, ALU.mult, [Bmods, Bcols], [Bdrv])
                    vcopy(drv[:, l, sub, 1, tc, :], mods[:, l, ish * 8:(ish + 1) * 8, tc], [Bmods], [Bdrv])
                    ts(drv[:, l, sub, 2, tc, :], mods[:, l, ig * 8:(ig + 1) * 8, tc], gmul, None, ALU.mult, None,
                       [Bmods], [Bdrv])
            tt(prod[:, 0:1], lamc[:, l * 4 + 0:l * 4 + 1], lamc[:, l * 4 + 1:l * 4 + 2], ALU.mult, [Blamc], [Bprod])
            tt(prod[:, 1:2], lamc[:, l * 4 + 2:l * 4 + 3], lamc[:, l * 4 + 3:l * 4 + 4], ALU.mult, [Blamc], [Bprod])
            mm(ps_t[:, 1, 0:2], ones_f[0:64, :], prod[:, :], True, True, [Bones_f, Bprod], [PB[1]])
            act(ee[:], ps_t[:, 1, 0:2], AF.Exp, [PB[1]], [Bee])
            tt(neglam[:, l:l + 1], ee[:, 1:2], ee[:, 0:1], ALU.subtract, [Bee], [Bneglam])
            ts(neglam[:, l:l + 1], neglam[:, l:l + 1], -LAM_INIT[l], None, ALU.add, None, [Bneglam], [Bneglam])
            ts(sublnc[:, l:l + 1], lcol(l, CL_SUBLN), 1.0 - LAM_INIT[l], None, ALU.mult, None, [Bcols], [Bsublnc])
        S.barrier()

    def DRV(l, sub, which, tc, k):
        return drv[:, l, sub, which, tc, k:k + 1]

    def rms_rstd(xt, Bx, nch, s0, ns, sq, Bsq, rstd, Brstd, bank, nfeat):
        act(sq[:, 0:nch, 0:ns], xt[:, 0:nch, s0:s0 + ns], AF.Square, [Bx], [Bsq])
        for k in range(nch):
            mm(ps_t[:, bank, 0:ns], ones_bf[:, :], sq[:, k, 0:ns], k == 0, k == nch - 1, [Bones_bf, Bsq], [PB[bank]])
        act(rstd[:, 0:ns], ps_t[:, bank, 0:ns], AF.Sqrt, [PB[bank]], [Brstd], bias=EPS, scale=1.0 / nfeat)
        recip(rstd[:, 0:ns], rstd[:, 0:ns], [Brstd], [Brstd])

    def norm_mod(xt, Bx, s0, ns, h, Bh, h0, l, sub, tc, sq, Bsq, rstd, Brstd, tmps, bank):
        rms_rstd(xt, Bx, 8, s0, ns, sq, Bsq, rstd, Brstd, bank, D)
        for k in range(8):
            tm, Btm = tmps[k % 2]
            tt(tm[:, 0:ns], xt[:, k, s0:s0 + ns], rstd[:, 0:ns], ALU.mult, [Bx, Brstd], [Btm])
            act(h[:, k, h0:h0 + ns], tm[:, 0:ns], AF.Identity, [Btm, Bdrv], [Bh],
                bias=DRV(l, sub, 1, tc, k), scale=DRV(l, sub, 0, tc, k))

    def ffn_phase(l, f, supers, src):
        sub = 0 if f == 0 else 2
        NMAX = 1280
        with ExitStack() as ph:
            xt, Bx = tile(ph, "f_x", [128, 8, NMAX], F32)
            sq, Bsq = tile(ph, "f_sq", [128, 8, 512], BF16)
            rstd, Brstd = tile(ph, "f_rstd", [128, 512], F32)
            tmps = [tile(ph, f"f_tmp{i}", [128, 512], F32) for i in range(2)]
            h, Bh = tile(ph, "f_h", [128, 8, NMAX], BF16)
            u, Bu = tile(ph, "f_u", [128, 22, NMAX], BF16)
            sls = [tile(ph, f"f_sl{i}", [128, 512], F32) for i in range(2)]
            w13 = [tile(ph, f"f_w13_{i}", [128, 2, 8, 128], BF16) for i in range(3)]
            w2 = [tile(ph, f"f_w2_{i}", [128, 22, 128], BF16) for i in range(3)]
            i13 = i2 = isl = 0
            pp = 0
            for su in supers:
                t0 = UNITS[su[0]][0]
                n = sum(UNITS[u_][1] for u_ in su)
                subs = [(UNITS[u_][0] - t0, UNITS[u_][1], 1 if u_ == 0 else 0) for u_ in su]
                dma("sp", xt[:, :, 0:n], src.ap.rearrange("(c p) t -> p c t", p=128)[:, :, t0:t0 + n],
                    src.b(None, su), [Bx])
                for (s0, ns, tc) in subs:
                    norm_mod(xt, Bx, s0, ns, h, Bh, s0, l, sub, tc, sq, Bsq, rstd, Brstd, tmps, 6)
                for j in range(22):
                    ws, Bws = w13[i13 % 3]
                    i13 += 1
                    dma("pool", ws[:].rearrange("p a k c -> p (a k c)"), w13s_d[l, f, j], [], [Bws])
                    for (s0, ns, tc) in subs:
                        ba, bb = (0, 1) if pp % 2 == 0 else (2, 3)
                        pp += 1
                        for k in range(8):
                            mm(ps_t[:, ba, 0:ns], ws[:, 0, k, :], h[:, k, s0:s0 + ns], k == 0, k == 7, [Bws, Bh], [PB[ba]])
                        for k in range(8):
                            mm(ps_t[:, bb, 0:ns], ws[:, 1, k, :], h[:, k, s0:s0 + ns], k == 0, k == 7, [Bws, Bh], [PB[bb]])
                        sl, Bsl = sls[isl % 2]
                        isl += 1
                        act(sl[:, 0:ns], ps_t[:, ba, 0:ns], AF.Silu, [PB[ba]], [Bsl])
                        tt(u[:, j, s0:s0 + ns], sl[:, 0:ns], ps_t[:, bb, 0:ns], ALU.mult, [Bsl, PB[bb]], [Bu])
                for m in range(8):
                    ws, Bws = w2[i2 % 3]
                    i2 += 1
                    dma("pool", ws[:].rearrange("p k c -> p (k c)"), w2s_d[l, f, m], [], [Bws])
                    for (s0, ns, tc) in subs:
                        bo = 4 + (pp % 2)
                        pp += 1
                        for k in range(22):
                            mm(ps_t[:, bo, 0:ns], ws[:, k, :], u[:, k, s0:s0 + ns], k == 0, k == 21, [Bws, Bu], [PB[bo]])
                        stt(xt[:, m, s0:s0 + ns], ps_t[:, bo, 0:ns], DRV(l, sub, 2, tc, m), xt[:, m, s0:s0 + ns],
                            ALU.mult, ALU.add, [PB[bo], Bdrv, Bx], [Bx])
                dma("sp", res.ap.rearrange("(c p) t -> p c t", p=128)[:, :, t0:t0 + n], xt[:, :, 0:n],
                    [Bx], res.b(None, su))
            S.barrier()

    def mixin_phase(l, supers, src):
        with ExitStack() as ph:
            xt, Bx = tile(ph, "m_x", [128, 8, 512], F32)
            sq, Bsq = tile(ph, "m_sq", [128, 8, 512], BF16)
            rstd, Brstd = tile(ph, "m_rstd", [128, 512], F32)
            tmps = [tile(ph, f"m_tmp{i}", [128, 512], F32) for i in range(2)]
            h, Bh = tile(ph, "m_h", [128, 8, 1024], BF16)
            uT, BuT = tile(ph, "m_uT", [128, 8, 1024], BF16)
            wf = [tile(ph, f"m_wf{i}", [128, 8, 128], BF16) for i in range(4)]
            wt = [tile(ph, f"m_wt{i}", [128, 8, 512], BF16) for i in range(2)]
            cosB, BcosB = tile(ph, "m_cosB", [128, 1024], F32)
            sinB, BsinB = tile(ph, "m_sinB", [128, 1024], F32)
            cosC, BcosC = tile(ph, "m_cosC", [96, 1024], F32)
            sinC, BsinC = tile(ph, "m_sinC", [96, 1024], F32)
            stg = [tile(ph, f"m_stg{i}", [128, 1024], BF16) for i in range(3)]
            r1, Br1 = tile(ph, "m_r1", [128, 512], F32)
            r2, Br2 = tile(ph, "m_r2", [128, 512], F32)
            cq, Bcq = tile(ph, "m_cq", [128, 4, 512], F32)
            cqn, Bcqn = tile(ph, "m_cqn", [128, 4, 512], BF16)
            ckv, Bckv = tile(ph, "m_ckv", [128, 2, 512], F32)
            ckvn, Bckvn = tile(ph, "m_ckvn", [128, 2, 512], BF16)
            vtok, Bvtok = tile(ph, "m_vtok", [128, 1024], F32)
            vn, Bvn = tile(ph, "m_vn", [128, 1024], BF16)
            ztmp, Bztmp = tile(ph, "m_ztmp", [128, 1024], F32)
            st1, Bst1 = tile(ph, "m_st1", [128, 8], F32)
            lng, Blng = tile(ph, "m_lng", [128, 1024], F32)
            lnb, Blnb = tile(ph, "m_lnb", [128, 1024], F32)
            bsb, Bbsb = tile(ph, "m_bsb", [128, 1024], F32)
            wsT, BwsT = tile(ph, "m_wsT", [128, 8, 128], BF16)
            vstg = [tile(ph, f"m_vstg{i}", [128, 1024], BF16) for i in range(2)]
            wq = [tile(ph, f"m_wq{i}", [128, 4, 192], BF16) for i in range(3)]
            wkk = [tile(ph, f"m_wkk{i}", [128, 2, 128], BF16) for i in range(2)]
            wkv = [tile(ph, f"m_wkv{i}", [128, 2, 512], BF16) for i in range(2)]
            cnt = {"wf": 0, "stg": 0, "vstg": 0, "wq": 0, "wkk": 0, "pp": 0}

            dma("sp", lng[:], lnv_d[l, 0], [], [Blng])
            dma("sp", lnb[:], lnv_d[l, 1], [], [Blnb])
            dma("sp", bsb[:], bsb_d[l], [], [Bbsb])
            dma("pool", wsT[:].rearrange("p g i -> p (g i)"), wsT_d[l], [], [BwsT])

            def next_wf(ch):
                ws, Bws = wf[cnt["wf"] % 4]
                cnt["wf"] += 1
                dma("pool", ws[:].rearrange("p k c -> p (k c)"), winf_d[l, ch], [], [Bws])
                return ws, Bws

            def next_stg():
                s_ = stg[cnt["stg"] % 3]
                cnt["stg"] += 1
                return s_

            def bank2():
                b_ = (cnt["pp"] % 2) * 2
                cnt["pp"] += 1
                return b_, b_ + 1

            def proj(ws, Bws, s0, ns, bank, mcols=128):
                for k in range(8):
                    mm(ps_t[0:mcols, bank, 0:ns], ws[:, k, 0:mcols], h[:, k, s0:s0 + ns], k == 0, k == 7,
                       [Bws, Bh], [PB[bank]])

            for su in supers:
                t0 = UNITS[su[0]][0]
                n = sum(UNITS[u_][1] for u_ in su)
                is_ctx = su[0] == 0
                tc = 1 if is_ctx else 0
                subs = [(UNITS[u_][0] - t0, UNITS[u_][1], u_) for u_ in su]
                if not is_ctx:
                    p0 = t0 - CTX
                    dma("sp", cosB[:, 0:n], ropeB_d[0, :, p0:p0 + n], [], [BcosB])
                    dma("sp", sinB[:, 0:n], ropeB_d[1, :, p0:p0 + n], [], [BsinB])
                    dma("sp", cosC[64:96, 0:n], ropeC_d[0, :, p0:p0 + n], [], [BcosC])
                    dma("sp", sinC[64:96, 0:n], ropeC_d[1, :, p0:p0 + n], [], [BsinC])
                for (s0, ns, u_) in subs:
                    dma("sp", xt[:, :, 0:ns], src.ap.rearrange("(c p) t -> p c t", p=128)[:, :, t0 + s0:t0 + s0 + ns],
                        src.b(None, [u_]), [Bx])
                    norm_mod(xt, Bx, 0, ns, h, Bh, s0, l, 1, tc, sq, Bsq, rstd, Brstd, tmps, 6)
                for fc in range(8):
                    ws, Bws = next_wf(WF_U + fc)
                    for (s0, ns, u_) in subs:
                        b0, _ = bank2()
                        proj(ws, Bws, s0, ns, b0)
                        act(uT[:, fc, s0:s0 + ns], ps_t[:, b0, 0:ns], AF.Gelu_apprx_tanh, [PB[b0]], [BuT])
                for i in range(2):
                    dma("pool", wt[i][0][:].rearrange("p k c -> p (k c)"), wint_d[l, i], [], [wt[i][1]])
                for tt_ in range(n // 128):
                    tk = slice(tt_ * 128, (tt_ + 1) * 128)
                    for nt in range(2):
                        for k in range(8):
                            mm(ps_t[:, nt, :], h[:, k, tk], wt[nt][0][:, k, :], k == 0, k == 7, [Bh, wt[nt][1]], [PB[nt]])
                        act(vtok[:, nt * 512:(nt + 1) * 512], ps_t[:, nt, :], AF.Gelu_apprx_tanh, [PB[nt]], [Bvtok])
                    S.op("dve", lambda e: e.reduce_sum(out=st1[:, 0:1], in_=vtok[:, :], axis=AX.X), reads=[Bvtok], writes=[Bst1])
                    act(ztmp[:, :], vtok[:, :], AF.Square, [Bvtok], [Bztmp])
                    S.op("dve", lambda e: e.reduce_sum(out=st1[:, 1:2], in_=ztmp[:, :], axis=AX.X), reads=[Bztmp], writes=[Bst1])
                    ts(st1[:, 2:3], st1[:, 0:1], -1.0 / 1024, None, ALU.mult, None, [Bst1], [Bst1])
                    tt(st1[:, 3:4], st1[:, 2:3], st1[:, 2:3], ALU.mult, [Bst1], [Bst1])
                    stt(st1[:, 4:5], st1[:, 1:2], 1.0 / 1024, st1[:, 3:4], ALU.mult, ALU.subtract, [Bst1], [Bst1])
                    act(st1[:, 5:6], st1[:, 4:5], AF.Sqrt, [Bst1], [Bst1], bias=EPS, scale=1.0)
                    recip(st1[:, 5:6], st1[:, 5:6], [Bst1], [Bst1])
                    ts(vtok[:, :], vtok[:, :], st1[:, 2:3], st1[:, 5:6], ALU.add, ALU.mult, [Bvtok, Bst1], [Bvtok])
                    tt(vtok[:, :], vtok[:, :], lng[:, :], ALU.mult, [Bvtok, Blng], [Bvtok])
                    tt(vn[:, :], vtok[:, :], lnb[:, :], ALU.add, [Bvtok, Blnb], [Bvn])
                    for g in range(8):
                        mm(ps_t[:, 2 + g // 4, (g % 4) * 128:(g % 4 + 1) * 128], vn[:, g * 128:(g + 1) * 128], wsT[:, g, :],
                           True, True, [Bvn, BwsT], [PB[2 + g // 4]])
                    tt(ztmp[:, :], ps_t[:, 2:4, :].rearrange("p a b -> p (a b)"), bsb[:, :], ALU.add,
                       [PB[2], PB[3], Bbsb], [Bztmp])
                    tt(uT[:, :, tk], ztmp[:, :].rearrange("p (g i) -> p g i", g=8), uT[:, :, tk], ALU.mult,
                       [Bztmp, BuT], [BuT])
                dma("sp", oaT.ap.rearrange("(c p) t -> p c t", p=128)[:, :, t0:t0 + n], uT[:, :, 0:n], [BuT], oaT.b(None, su))
                for (wmain, wswap, dst) in ((WF_QB, WF_QBS, qbT), (WF_KB, WF_KBS, kbT)):
                    for fc in range(8):
                        ws, Bws = next_wf(wmain + fc)
                        if not is_ctx:
                            ws2, Bws2 = next_wf(wswap + fc)
                        sg, Bsg = next_stg()
                        for (s0, ns, u_) in subs:
                            b0, b1 = bank2()
                            proj(ws, Bws, s0, ns, b0)
                            if is_ctx:
                                act(sg[:, s0:s0 + ns], ps_t[:, b0, 0:ns], AF.Identity, [PB[b0]], [Bsg])
                            else:
                                proj(ws2, Bws2, s0, ns, b1)
                                tt(r1[:, 0:ns], ps_t[:, b0, 0:ns], cosB[:, s0:s0 + ns], ALU.mult, [PB[b0], BcosB], [Br1])
                                tt(r2[:, 0:ns], ps_t[:, b1, 0:ns], sinB[:, s0:s0 + ns], ALU.mult, [PB[b1], BsinB], [Br2])
                                tt(sg[:, s0:s0 + ns], r1[:, 0:ns], r2[:, 0:ns], ALU.add, [Br1, Br2], [Bsg])
                        dma("sp", dst.ap[fc * 128:(fc + 1) * 128, t0:t0 + n], sg[:, 0:n], [Bsg], dst.b([fc], su))
                for i in range(2):
                    dma("pool", wt[i][0][:].rearrange("p k c -> p (k c)"), wint_d[l, 2 + i], [], [wt[i][1]])
                for tt_ in range(n // 128):
                    tk = slice(tt_ * 128, (tt_ + 1) * 128)
                    vs, Bvs = vstg[cnt["vstg"] % 2]
                    cnt["vstg"] += 1
                    for nt in range(2):
                        for k in range(8):
                            mm(ps_t[:, nt, :], h[:, k, tk], wt[nt][0][:, k, :], k == 0, k == 7, [Bh, wt[nt][1]], [PB[nt]])
                        if nt == 0:
                            act(vs[:, 0:512], ps_t[:, 0, :], AF.Identity, [PB[0]], [Bvs])
                        else:
                            vcopy(vs[:, 512:1024], ps_t[:, 1, :], [PB[1]], [Bvs])
                    uu = su[(tt_ * 128) // 512] if not is_ctx else 0
                    dma("sp", vbt.ap[t0 + tt_ * 128:t0 + (tt_ + 1) * 128, :], vs[:, :], [Bvs], vbt.b(None, [uu]))
                for (s0, ns, u_) in subs:
                    for fc in range(4):
                        ws, Bws = next_wf(WF_CQ + fc)
                        b0, _ = bank2()
                        proj(ws, Bws, s0, ns, b0)
                        act(cq[:, fc, 0:ns], ps_t[:, b0, 0:ns], AF.Identity, [PB[b0]], [Bcq])
                    for fc in range(2):
                        ws, Bws = next_wf(WF_CKV + fc)
                        b0, _ = bank2()
                        proj(ws, Bws, s0, ns, b0)
                        vcopy(ckv[:, fc, 0:ns], ps_t[:, b0, 0:ns], [PB[b0]], [Bckv])
                    rms_rstd(cq, Bcq, 4, 0, ns, sq, Bsq, rstd, Brstd, 6, 512)
                    for fc in range(4):
                        stt(cqn[:, fc, 0:ns], cq[:, fc, 0:ns], lcol(l, CL_QNG + fc), rstd[:, 0:ns], ALU.mult, ALU.mult,
                            [Bcq, Bcols, Brstd], [Bcqn])
                    rms_rstd(ckv, Bckv, 2, 0, ns, sq, Bsq, rstd, Brstd, 6, 256)
                    for fc in range(2):
                        stt(ckvn[:, fc, 0:ns], ckv[:, fc, 0:ns], lcol(l, CL_KVNG + fc), rstd[:, 0:ns], ALU.mult, ALU.mult,
                            [Bckv, Bcols, Brstd], [Bckvn])
                    for hh in range(16):
                        ws, Bws = wq[cnt["wq"] % 3]
                        cnt["wq"] += 1
                        dma("pool", ws[:].rearrange("p k c -> p (k c)"), wuq_d[l, hh], [], [Bws])
                        sg, Bsg = next_stg()
                        b0, b1 = bank2()
                        for k in range(4):
                            mm(ps_t[0:96, b0, 0:ns], ws[:, k, 0:96], cqn[:, k, 0:ns], k == 0, k == 3, [Bws, Bcqn], [PB[b0]])
                        if is_ctx:
                            act(sg[0:96, 0:ns], ps_t[0:96, b0, 0:ns], AF.Identity, [PB[b0]], [Bsg])
                        else:
                            for k in range(4):
                                mm(ps_t[0:96, b1, 0:ns], ws[:, k, 96:192], cqn[:, k, 0:ns], k == 0, k == 3, [Bws, Bcqn], [PB[b1]])
                            act(sg[0:64, 0:ns], ps_t[0:64, b0, 0:ns], AF.Identity, [PB[b0]], [Bsg])
                            tt(r1[64:96, 0:ns], ps_t[64:96, b0, 0:ns], cosC[64:96, s0:s0 + ns], ALU.mult, [PB[b0], BcosC], [Br1])
                            tt(r2[64:96, 0:ns], ps_t[64:96, b1, 0:ns], sinC[64:96, s0:s0 + ns], ALU.mult, [PB[b1], BsinC], [Br2])
                            tt(sg[64:96, 0:ns], r1[64:96, 0:ns], r2[64:96, 0:ns], ALU.add, [Br1, Br2], [Bsg])
                        dma("sp", qcT.ap[hh * 96:(hh + 1) * 96, t0 + s0:t0 + s0 + ns], sg[0:96, 0:ns], [Bsg], qcT.b([hh], [u_]))
                    for hp in range(8):
                        ws, Bws = wkk[cnt["wkk"] % 2]
                        cnt["wkk"] += 1
                        dma("pool", ws[:].rearrange("p k c -> p (k c)"), wukvk_d[l, hp], [], [Bws])
                        sg, Bsg = next_stg()
                        b0, _ = bank2()
                        for k in range(2):
                            mm(ps_t[:, b0, 0:ns], ws[:, k, :], ckvn[:, k, 0:ns], k == 0, k == 1, [Bws, Bckvn], [PB[b0]])
                        act(sg[:, 0:ns], ps_t[:, b0, 0:ns], AF.Identity, [PB[b0]], [Bsg])
                        dma("sp", kncT.ap[hp * 128:(hp + 1) * 128, t0 + s0:t0 + s0 + ns], sg[:, 0:ns], [Bsg], kncT.b([hp], [u_]))
                    for i in range(2):
                        dma("pool", wkv[i][0][:].rearrange("p k c -> p (k c)"), wukvv_d[l, i], [], [wkv[i][1]])
                    for tt_ in range(ns // 128):
                        tk = slice(tt_ * 128, (tt_ + 1) * 128)
                        vs, Bvs = vstg[cnt["vstg"] % 2]
                        cnt["vstg"] += 1
                        for nt in range(2):
                            for k in range(2):
                                mm(ps_t[:, nt, :], ckvn[:, k, tk], wkv[nt][0][:, k, :], k == 0, k == 1, [Bckvn, wkv[nt][1]], [PB[nt]])
                            if nt == 0:
                                act(vs[:, 0:512], ps_t[:, 0, :], AF.Identity, [PB[0]], [Bvs])
                            else:
                                vcopy(vs[:, 512:1024], ps_t[:, 1, :], [PB[1]], [Bvs])
                        ta = t0 + s0 + tt_ * 128
                        dma("sp", vct.ap[ta:ta + 128, :], vs[:, :], [Bvs], vct.b(None, [u_]))
                    ws, Bws = next_wf(WF_KR)
                    sg, Bsg = next_stg()
                    b0, b1 = bank2()
                    proj(ws, Bws, s0, ns, b0, mcols=96)
                    if is_ctx:
                        act(sg[64:96, 0:ns], ps_t[64:96, b0, 0:ns], AF.Identity, [PB[b0]], [Bsg])
                    else:
                        ws2, Bws2 = next_wf(WF_KRS)
                        proj(ws2, Bws2, s0, ns, b1, mcols=96)
                        tt(r1[64:96, 0:ns], ps_t[64:96, b0, 0:ns], cosC[64:96, s0:s0 + ns], ALU.mult, [PB[b0], BcosC], [Br1])
                        tt(r2[64:96, 0:ns], ps_t[64:96, b1, 0:ns], sinC[64:96, s0:s0 + ns], ALU.mult, [PB[b1], BsinC], [Br2])
                        tt(sg[64:96, 0:ns], r1[64:96, 0:ns], r2[64:96, 0:ns], ALU.add, [Br1, Br2], [Bsg])
                    dma("sp", krT.ap[:, t0 + s0:t0 + s0 + ns], sg[64:96, 0:ns], [Bsg], krT.b(None, [u_]))
                for fc in range(24):
                    ws, Bws = next_wf(WF_G + fc)
                    sg, Bsg = next_stg()
                    for (s0, ns, u_) in subs:
                        b0, _ = bank2()
                        proj(ws, Bws, s0, ns, b0)
                        act(sg[:, s0:s0 + ns], ps_t[:, b0, 0:ns], AF.Sigmoid, [PB[b0], Bcols], [Bsg], bias=lcol(l, CL_BG + fc))
                    dma("sp", gT.ap[fc * 128:(fc + 1) * 128, t0:t0 + n], sg[:, 0:n], [Bsg], gT.b(None, su))
            S.barrier()

    DEFER_KT = 12

    def diff_phase(l, with_ctx_q):
        scale = 64 ** -0.5
        with ExitStack() as ph:
            kT, BkT = tile(ph, "d_kT", [128, T], BF16)
            vt, Bvt = tile(ph, "d_vt", [128, NKT, 128], BF16)
            qs = [tile(ph, f"d_q{i}", [128, 512], BF16) for i in range(2)]
            p1s = [tile(ph, f"d_p1_{i}", [128, 512], BF16) for i in range(3)]
            p2s = [tile(ph, f"d_p2_{i}", [128, 512], BF16) for i in range(3)]
            accs = [[tile(ph, f"d_acc{i}_{j}", [128, 512], F32) for j in range(4)] for i in range(2)]
            sqs = [tile(ph, f"d_sq{i}", [128, 512], BF16) for i in range(2)]
            lnt, Blnt = tile(ph, "d_lnt", [128, 512], F32)
            rstd, Brstd = tile(ph, "d_rstd", [128, 512], F32)
            outs = [tile(ph, f"d_out{i}", [128, 512], BF16) for i in range(2)]
            blocks = [(u_, NKT) for u_ in range(1, 9)]
            if with_ctx_q:
                blocks = [(0, 2)] + blocks
            cnt = {"q": 0, "p": 0, "o": 0, "blk": 0}
            pending = []

            def flush(bank):
                while pending:
                    pending.pop(0)(bank)

            for hh in range(8):
                dma("sp", kT[:, :], kbT.ap[hh * 128:(hh + 1) * 128, :], kbT.b([hh], None), [BkT])
                dma("sp", vt[:, :, :], vbt.ap.rearrange("(kt p) e -> p kt e", p=128)[:, :, hh * 128:(hh + 1) * 128],
                    vbt.b(None, None), [Bvt])
                for (u_, nkt) in blocks:
                    t0, nq = UNITS[u_]
                    qt, Bqt = qs[cnt["q"] % 2]
                    cnt["q"] += 1
                    dma("sp", qt[:, 0:nq], qbT.ap[hh * 128:(hh + 1) * 128, t0:t0 + nq], qbT.b([hh], [u_]), [Bqt])

                    def scores(kt):
                        ks = slice(kt * 128, (kt + 1) * 128)
                        ba, bb = (0, 1) if kt % 2 == 0 else (2, 3)
                        mm(ps_t[:, ba, 0:nq], kT[0:64, ks], qt[0:64, 0:nq], True, True, [BkT, Bqt], [PB[ba]])
                        mm(ps_t[:, bb, 0:nq], kT[64:128, ks], qt[64:128, 0:nq], True, True, [BkT, Bqt], [PB[bb]])

                    scores(0)
                    for kt in range(nkt):
                        if kt + 1 < nkt:
                            scores(kt + 1)
                        ba, bb = (0, 1) if kt % 2 == 0 else (2, 3)
                        p1, Bp1 = p1s[cnt["p"] % 3]
                        p2, Bp2 = p2s[cnt["p"] % 3]
                        cnt["p"] += 1
                        act(p1[:, 0:nq], ps_t[:, ba, 0:nq], AF.Exp, [PB[ba]], [Bp1], scale=scale)
                        act(p2[:, 0:nq], ps_t[:, bb, 0:nq], AF.Exp, [PB[bb]], [Bp2], scale=scale)
                        st, sp_ = kt == 0, kt == nkt - 1
                        mm(ps_t[:, 4, 0:nq], vt[:, kt, :], p1[:, 0:nq], st, sp_, [Bvt, Bp1], [PB[4]])
                        mm(ps_t[:, 5, 0:nq], vt[:, kt, :], p2[:, 0:nq], st, sp_, [Bvt, Bp2], [PB[5]])
                        mm(ps_t[:, 6, 0:nq], ones_bf[:, :], p1[:, 0:nq], st, sp_, [Bones_bf, Bp1], [PB[6]])
                        mm(ps_t[:, 7, 0:nq], ones_bf[:, :], p2[:, 0:nq], st, sp_, [Bones_bf, Bp2], [PB[7]])
                        if kt == DEFER_KT or (kt == nkt - 1 and nkt <= DEFER_KT):
                            flush(0 if kt % 2 == 0 else 2)
                    (o1, Bo1), (o2, Bo2), (s1, Bs1), (s2, Bs2) = accs[cnt["blk"] % 2]
                    sq, Bsq = sqs[cnt["blk"] % 2]
                    cnt["blk"] += 1
                    act(o1[:, 0:nq], ps_t[:, 4, 0:nq], AF.Identity, [PB[4]], [Bo1])
                    vcopy(o2[:, 0:nq], ps_t[:, 5, 0:nq], [PB[5]], [Bo2])
                    act(s1[:, 0:nq], ps_t[:, 6, 0:nq], AF.Identity, [PB[6]], [Bs1])
                    vcopy(s2[:, 0:nq], ps_t[:, 7, 0:nq], [PB[7]], [Bs2])
                    recip(s1[:, 0:nq], s1[:, 0:nq], [Bs1], [Bs1])
                    recip(s2[:, 0:nq], s2[:, 0:nq], [Bs2], [Bs2])
                    tt(o1[:, 0:nq], o1[:, 0:nq], s1[:, 0:nq], ALU.mult, [Bo1, Bs1], [Bo1])
                    tt(o2[:, 0:nq], o2[:, 0:nq], s2[:, 0:nq], ALU.mult, [Bo2, Bs2], [Bo2])
                    stt(o1[:, 0:nq], o2[:, 0:nq], neglam[:, l:l + 1], o1[:, 0:nq], ALU.mult, ALU.add,
                        [Bo2, Bneglam, Bo1], [Bo1])
                    tt(sq[:, 0:nq], o1[:, 0:nq], o1[:, 0:nq], ALU.mult, [Bo1], [Bsq])

                    def part2b(bank, o1=o1, Bo1=Bo1, sq=sq, Bsq=Bsq, nq=nq, t0=t0, hh=hh, u_=u_):
                        mm(ps_t[:, bank, 0:nq], ones_bf[:, :], sq[:, 0:nq], True, True, [Bones_bf, Bsq], [PB[bank]])
                        act(lnt[:, 0:nq], ps_t[:, bank, 0:nq], AF.Ln, [PB[bank]], [Blnt], bias=EPS, scale=1.0 / 128)
                        act(rstd[:, 0:nq], lnt[:, 0:nq], AF.Exp, [Blnt], [Brstd], scale=-0.5)
                        ot, Bot = outs[cnt["o"] % 2]
                        cnt["o"] += 1
                        stt(ot[:, 0:nq], o1[:, 0:nq], sublnc[:, l:l + 1], rstd[:, 0:nq], ALU.mult, ALU.mult,
                            [Bo1, Bsublnc, Brstd], [Bot])
                        dma("sp", obT.ap[hh * 128:(hh + 1) * 128, t0:t0 + nq], ot[:, 0:nq], [Bot], obT.b([hh], [u_]))

                    pending.append(part2b)
            flush(0)
            S.barrier()

    def mla_phase(l, with_ctx_q):
        scale = 96 ** -0.5
        with ExitStack() as ph:
            kTs = [tile(ph, f"c_kT{i}", [96, T], BF16) for i in range(2)]
            vts = [tile(ph, f"c_vt{i}", [128, NKT, 65], BF16) for i in range(2)]
            qs = [tile(ph, f"c_q{i}", [96, 512], BF16) for i in range(2)]
            ps_ = [tile(ph, f"c_p{i}", [128, 512], BF16) for i in range(4)]
            osbs = [tile(ph, f"c_osb{i}", [65, 512], F32) for i in range(2)]
            outs = [tile(ph, f"c_out{i}", [64, 512], BF16) for i in range(2)]
            blocks = [(u_, NKT) for u_ in range(1, 9)]
            if with_ctx_q:
                blocks = [(0, 2)] + blocks
            for i in range(2):
                memset("pool", vts[i][0][:, :, 64:65], 1.0, [vts[i][1]])
            cnt = {"q": 0, "p": 0, "o": 0, "blk": 0}
            pending = []

            def flush():
                while pending:
                    pending.pop(0)()

            for hh in range(16):
                kT, BkT = kTs[hh % 2]
                vt, Bvt = vts[hh % 2]
                dma("sp", kT[0:64, :], kncT.ap[hh * 64:(hh + 1) * 64, :], kncT.b([hh // 2], None), [BkT])
                dma("sp", kT[64:96, :], krT.ap[:, :], krT.b(None, None), [BkT])
                dma("sp", vt[:, :, 0:64], vct.ap.rearrange("(kt p) e -> p kt e", p=128)[:, :, hh * 64:(hh + 1) * 64],
                    vct.b(None, None), [Bvt])
                for (u_, nkt) in blocks:
                    t0, nq = UNITS[u_]
                    qt, Bqt = qs[cnt["q"] % 2]
                    cnt["q"] += 1
                    dma("sp", qt[:, 0:nq], qcT.ap[hh * 96:(hh + 1) * 96, t0:t0 + nq], qcT.b([hh], [u_]), [Bqt])
                    bo = 4 + (cnt["blk"] % 2)
                    osb, Bosb = osbs[cnt["blk"] % 2]
                    cnt["blk"] += 1

                    def scores(kt):
                        ks = slice(kt * 128, (kt + 1) * 128)
                        mm(ps_t[:, kt % 4, 0:nq], kT[0:96, ks], qt[0:96, 0:nq], True, True, [BkT, Bqt], [PB[kt % 4]])

                    scores(0)
                    if nkt > 1:
                        scores(1)
                    for kt in range(nkt):
                        if kt + 2 < nkt:
                            scores(kt + 2)
                        p, Bp = ps_[cnt["p"] % 4]
                        cnt["p"] += 1
                        act(p[:, 0:nq], ps_t[:, kt % 4, 0:nq], AF.Exp, [PB[kt % 4]], [Bp], scale=scale)
                        mm(ps_t[0:65, bo, 0:nq], vt[:, kt, :], p[:, 0:nq], kt == 0, kt == nkt - 1, [Bvt, Bp], [PB[bo]])
                        if kt == DEFER_KT or (kt == nkt - 1 and nkt <= DEFER_KT):
                            flush()
                    vcopy(osb[0:65, 0:nq], ps_t[0:65, bo, 0:nq], [PB[bo]], [Bosb])
                    recip(osb[64:65, 0:nq], osb[64:65, 0:nq], [Bosb], [Bosb])

                    def part2(osb=osb, Bosb=Bosb, nq=nq, t0=t0, hh=hh, u_=u_):
                        mm(ps_t[0:64, 6, 0:nq], ones_f[64:65, 0:64], osb[64:65, 0:nq], True, True, [Bones_f, Bosb], [PB[6]])
                        ot, Bot = outs[cnt["o"] % 2]
                        cnt["o"] += 1
                        tt(ot[:, 0:nq], osb[0:64, 0:nq], ps_t[0:64, 6, 0:nq], ALU.mult, [Bosb, PB[6]], [Bot])
                        dma("sp", ocT.ap[hh * 64:(hh + 1) * 64, t0:t0 + nq], ot[:, 0:nq], [Bot], ocT.b([hh], [u_]))

                    pending.append(part2)
            flush()
            S.barrier()

    def merge_phase(l, units):
        with ExitStack() as ph:
            oin = [tile(ph, f"g_oin{i}", [128, 8, 512], BF16) for i in range(2)]
            gss = [tile(ph, f"g_gs{i}", [128, 24, 512], BF16) for i in range(2)]
            y, By = tile(ph, "g_y", [128, 8, 512], F32)
            ybf, Bybf = tile(ph, "g_ybf", [128, 8, 512], BF16)
            xts = [tile(ph, f"g_x{i}", [128, 8, 512], F32) for i in range(2)]
            tg = [tile(ph, f"g_tg{i}", [128, 512], F32) for i in range(2)]
            wall, Bwall = tile(ph, "g_wall", [128, 32, 8, 128], BF16)
            Bw = [Buf(f"g_w{i}") for i in range(32)]
            for br in range(3):
                for m in range(8):
                    dma("pool", wall[:, br * 8 + m, :, :].rearrange("p k c -> p (k c)"), wbr_d[l, br, m], [], [Bw[br * 8 + m]])
            for m in range(8):
                dma("pool", wall[:, 24 + m, :, :].rearrange("p k c -> p (k c)"), wout_d[l, m], [], [Bw[24 + m]])
            io = it = pp = iu = 0
            for u_ in units:
                t0, n = UNITS[u_]
                tc = 1 if u_ == 0 else 0
                gs, Bgs = gss[iu % 2]
                xt, Bx = xts[iu % 2]
                iu += 1
                dma("sp", gs[:, :, 0:n], gT.ap.rearrange("(c p) t -> p c t", p=128)[:, :, t0:t0 + n], gT.b(None, [u_]), [Bgs])
                dma("sp", xt[:, :, 0:n], res.ap.rearrange("(c p) t -> p c t", p=128)[:, :, t0:t0 + n], res.b(None, [u_]), [Bx])
                for br, src in enumerate((oaT, obT, ocT)):
                    ot, Bot = oin[io % 2]
                    io += 1
                    dma("sp", ot[:, :, 0:n], src.ap.rearrange("(c p) t -> p c t", p=128)[:, :, t0:t0 + n],
                        src.b(None, [u_]), [Bot])
                    for m in range(8):
                        bk = pp % 4
                        pp += 1
                        wi = br * 8 + m
                        for k in range(8):
                            mm(ps_t[:, bk, 0:n], wall[:, wi, k, :], ot[:, k, 0:n], k == 0, k == 7, [Bw[wi], Bot], [PB[bk]])
                        if br == 0:
                            tt(y[:, m, 0:n], ps_t[:, bk, 0:n], gs[:, br * 8 + m, 0:n], ALU.mult, [PB[bk], Bgs], [By])
                        else:
                            tm, Btm = tg[it % 2]
                            it += 1
                            tt(tm[:, 0:n], ps_t[:, bk, 0:n], gs[:, br * 8 + m, 0:n], ALU.mult, [PB[bk], Bgs], [Btm])
                            tt(y[:, m, 0:n], y[:, m, 0:n], tm[:, 0:n], ALU.add, [By, Btm], [By])
                act(ybf[:, :, 0:n], y[:, :, 0:n], AF.Identity, [By], [Bybf])
                for m in range(8):
                    bk = pp % 4
                    pp += 1
                    for k in range(8):
                        mm(ps_t[:, bk, 0:n], wall[:, 24 + m, k, :], ybf[:, k, 0:n], k == 0, k == 7, [Bw[24 + m], Bybf], [PB[bk]])
                    stt(xt[:, m, 0:n], ps_t[:, bk, 0:n], DRV(l, 1, 2, tc, m), xt[:, m, 0:n], ALU.mult, ALU.add,
                        [PB[bk], Bdrv, Bx], [Bx])
                dma("sp", res.ap.rearrange("(c p) t -> p c t", p=128)[:, :, t0:t0 + n], xt[:, :, 0:n], [Bx], res.b(None, [u_]))
            S.barrier()

    def final_phase():
        with ExitStack() as ph:
            xt, Bx = tile(ph, "z_x", [128, 8, 512], F32)
            sq, Bsq = tile(ph, "z_sq", [128, 8, 512], BF16)
            rstd, Brstd = tile(ph, "z_rstd", [128, 512], F32)
            yo = [tile(ph, f"z_y{i}", [128, 8, 512], F32) for i in range(2)]
            for u_ in range(1, 9):
                t0, n = UNITS[u_]
                dma("sp", xt[:, :, :], res.ap.rearrange("(c p) t -> p c t", p=128)[:, :, t0:t0 + n], res.b(None, [u_]), [Bx])
                rms_rstd(xt, Bx, 8, 0, n, sq, Bsq, rstd, Brstd, u_ % 2, D)
                yt, Byt = yo[u_ % 2]
                for k in range(8):
                    stt(yt[:, k, :], xt[:, k, :], cols[:, C_NFIN + k:C_NFIN + k + 1], rstd[:, :], ALU.mult, ALU.mult,
                        [Bx, Bcols, Brstd], [Byt])
                dma("sp", out_d.ap.rearrange("(c p) t -> p c t", p=128)[:, :, t0 - CTX:t0 - CTX + n], yt[:, :, :],
                    [Byt], out_d.b(None, [u_]))
            S.op("sp", lambda e: e.nop(), reads=out_d.b(None, range(1, 9)))
            S.barrier()

    phases = []
    for l in range(NL):
        last = l == NL - 1
        phases.append(lambda l=l: ffn_phase(l, 0, SUPER_F, x0 if l == 0 else res))
        phases.append(lambda l=l: mixin_phase(l, SUPER, res))
        phases.append(lambda l=l, last=last: diff_phase(l, not last))
        phases.append(lambda l=l, last=last: mla_phase(l, not last))
        phases.append(lambda l=l, last=last: merge_phase(l, range(1, 9) if last else range(0, 9)))
        phases.append(lambda l=l, last=last: ffn_phase(l, 1, SUPER_FL if last else SUPER_F, res))
    phases.append(final_phase)
    for i, p in enumerate(phases):
        if stop_after is not None and i > stop_after:
            break
        p()
    S.emit(glob)
    glob.close()
    return nc


def _lhsT_slabs(W):
    K, N = W.shape
    kc, nch = K // 128, N // 128
    return np.ascontiguousarray(W.reshape(kc, 128, nch, 128).transpose(2, 1, 0, 3)).reshape(nch, 128, kc * 128)


def _tok_slab(W):
    K, N = W.shape
    kc = K // 128
    return np.ascontiguousarray(W.reshape(kc, 128, N).transpose(1, 0, 2)).reshape(128, kc * N)


def _col(v):
    return np.ascontiguousarray(np.asarray(v, np.float32).reshape(-1, 128).T)


def _rope_tables():
    rows = SEQ // 64
    row = np.repeat(np.arange(rows), 64).astype(np.float32)
    colp = np.tile(np.arange(64), rows).astype(np.float32)

    def tab(rot_dim):
        n_freq = rot_dim // 4
        freqs = (np.float32(10000.0) ** (-np.arange(n_freq, dtype=np.float32) / np.float32(n_freq))).astype(np.float32)
        ang = np.concatenate([row[:, None] * freqs, colp[:, None] * freqs], axis=-1).astype(np.float32)
        return np.cos(ang).astype(np.float32), np.sin(ang).astype(np.float32)

    cb, sb = tab(64)
    cc, sc = tab(32)
    ropeB = np.empty((2, 128, SEQ), np.float32)
    for p in range(128):
        d = p % 64
        ropeB[0, p] = cb[:, d % 32]
        ropeB[1, p] = (-sb[:, d % 32]) if d < 32 else sb[:, d % 32]
    ropeC = np.empty((2, 32, SEQ), np.float32)
    for d in range(32):
        ropeC[0, d] = cc[:, d % 16]
        ropeC[1, d] = (-sc[:, d % 16]) if d < 16 else sc[:, d % 16]
    return ropeB, ropeC


_PROGRAM = None


def _get_program():
    global _PROGRAM
    if _PROGRAM is None:
        _PROGRAM = build_program()
    return _PROGRAM


def kernel(x, c, ctx, c_ctx, ada_w, ada_b, norm_ffn1, ffn1_w13, ffn1_w2, norm_mix, w_in, b_gate,
           ln_v_g, ln_v_b, spatial_w, spatial_b, lambda_q1, lambda_k1, lambda_q2, lambda_k2, subln_g,
           q_norm_g, w_uq, kv_norm_g, w_ukv, w_branch, w_out, norm_ffn2, ffn2_w13, ffn2_w2, norm_final):
    f32 = np.float32
    A = lambda a: np.asarray(a, dtype=f32)
    x, c, ctx, c_ctx = A(x), A(c), A(ctx), A(c_ctx)
    ada_w, ada_b, w_in = A(ada_w), A(ada_b), A(w_in)
    w13 = [A(ffn1_w13), A(ffn2_w13)]
    w2 = [A(ffn1_w2), A(ffn2_w2)]
    w_uq, w_ukv, w_branch, w_out = A(w_uq), A(w_ukv), A(w_branch), A(w_out)
    spatial_w, spatial_b = A(spatial_w), A(spatial_b)
    B = x.shape[0]

    adaw = np.stack([np.ascontiguousarray(ada_w[l].reshape(8, 128, 9, 1024).transpose(2, 1, 0, 3)).reshape(9, 128, 8192)
                     for l in range(NL)])
    w13s = np.stack([np.stack([
        np.ascontiguousarray(w13[f][l].reshape(8, 128, 2, 22, 128).transpose(3, 1, 2, 0, 4)).reshape(22, 128, 2048)
        for f in range(2)]) for l in range(NL)])
    w2s = np.stack([np.stack([
        np.ascontiguousarray(w2[f][l].reshape(22, 128, 8, 128).transpose(2, 1, 0, 3)).reshape(8, 128, 2816)
        for f in range(2)]) for l in range(NL)])

    sw64 = (np.arange(1024) // 64) * 64 + (np.arange(1024) % 64 + 32) % 64
    sw32 = (np.arange(32) + 16) % 32
    winf, wint, wuq, wukvk, wukvv = [], [], [], [], []
    for l in range(NL):
        W = w_in[l]
        a_u, a_v = W[:, 0:1024], W[:, 1024:2048]
        qb, kb, vb = W[:, 2048:3072], W[:, 3072:4096], W[:, 4096:5120]
        cq, ckv, kr, g = W[:, 5120:5632], W[:, 5632:5888], W[:, 5888:5920], W[:, 5920:8992]
        krp = np.zeros((1024, 128), f32)
        krp[:, 64:96] = kr
        krs = np.zeros((1024, 128), f32)
        krs[:, 64:96] = kr[:, sw32]
        fam = np.concatenate([a_u, qb, qb[:, sw64], kb, kb[:, sw64], cq, ckv, krp, krs, g], axis=1)
        assert fam.shape[1] == WF_N * 128
        winf.append(_lhsT_slabs(fam))
        wint.append(np.stack([_tok_slab(a_v[:, 0:512]), _tok_slab(a_v[:, 512:1024]),
                              _tok_slab(vb[:, 0:512]), _tok_slab(vb[:, 512:1024])]))
        hq = []
        for h in range(16):
            blk = np.zeros((512, 192), f32)
            blk[:, 0:96] = w_uq[l][:, h * 96:(h + 1) * 96]
            blk[:, 160:192] = w_uq[l][:, h * 96 + 64 + sw32]
            hq.append(_tok_slab(blk))
        wuq.append(np.stack(hq))
        hk = []
        for hp in range(8):
            blk = np.concatenate([w_ukv[l][:, (2 * hp) * 128:(2 * hp) * 128 + 64],
                                  w_ukv[l][:, (2 * hp + 1) * 128:(2 * hp + 1) * 128 + 64]], axis=1)
            hk.append(_tok_slab(blk))
        wukvk.append(np.stack(hk))
        hv = []
        for nt in range(2):
            blk = np.concatenate([w_ukv[l][:, h * 128 + 64:h * 128 + 128] for h in range(nt * 8, nt * 8 + 8)], axis=1)
            hv.append(_tok_slab(blk))
        wukvv.append(np.stack(hv))
    winf, wint, wuq, wukvk, wukvv = map(np.stack, (winf, wint, wuq, wukvk, wukvv))
    wbr = np.stack([np.stack([_lhsT_slabs(w_branch[l, i]) for i in range(3)]) for l in range(NL)])
    wout = np.stack([_lhsT_slabs(w_out[l]) for l in range(NL)])
    wsT = np.stack([np.ascontiguousarray(spatial_w[l].transpose(2, 0, 1)).reshape(128, 1024) for l in range(NL)])
    lnv = np.stack([np.stack([np.broadcast_to(A(ln_v_g)[l][None, :], (128, 1024)),
                              np.broadcast_to(A(ln_v_b)[l][None, :], (128, 1024))]) for l in range(NL)]).astype(f32)
    bsb = np.stack([np.broadcast_to(spatial_b[l].reshape(1, 1024), (128, 1024)) for l in range(NL)]).astype(f32)
    ropeB, ropeC = _rope_tables()
    lamcols = np.stack([A(lambda_q1), A(lambda_k1), A(lambda_q2), A(lambda_k2)], axis=1)
    lamcols = np.ascontiguousarray(lamcols.transpose(2, 0, 1)).reshape(64, NL * 4)

    shared = dict(adaw=adaw, w13s=w13s, w2s=w2s, winf=winf, wint=wint, wuq=wuq, wukvk=wukvk, wukvv=wukvv,
                  wbr=wbr, wout=wout, wsT=wsT, lnv=lnv, bsb=bsb, ropeB=ropeB, ropeC=ropeC, lamcols=lamcols)
    shared = {k: np.ascontiguousarray(v, dtype=f32) for k, v in shared.items()}

    in_maps = []
    for b in range(B):
        cols = np.zeros((128, NCOL), f32)
        cols[:, C_C:C_C + 8] = _col(c[b])
        cols[:, C_CCTX:C_CCTX + 8] = _col(c_ctx)
        for l in range(NL):
            o = C_L0 + l * CL_N
            cols[:, o + CL_ADAB:o + CL_ADAB + 72] = _col(ada_b[l])
            cols[:, o + CL_NF1:o + CL_NF1 + 8] = _col(norm_ffn1[l])
            cols[:, o + CL_NMIX:o + CL_NMIX + 8] = _col(norm_mix[l])
            cols[:, o + CL_NF2:o + CL_NF2 + 8] = _col(norm_ffn2[l])
            cols[:, o + CL_BG:o + CL_BG + 24] = _col(b_gate[l])
            cols[:, o + CL_SUBLN:o + CL_SUBLN + 1] = _col(subln_g[l])
            cols[:, o + CL_QNG:o + CL_QNG + 4] = _col(q_norm_g[l])
            cols[:, o + CL_KVNG:o + CL_KVNG + 2] = _col(kv_norm_g[l])
        cols[:, C_NFIN:C_NFIN + 8] = _col(norm_final)
        xT0 = np.ascontiguousarray(np.concatenate([ctx[b].T, x[b].T], axis=1), dtype=f32)
        m = dict(shared)
        m["xT0"] = xT0
        m["cols"] = cols
        in_maps.append(m)

    nc = _get_program()
    res = run_bass_kernel_spmd(nc, in_maps, core_ids=list(range(B)))
    out = np.stack([np.ascontiguousarray(np.asarray(r["outT"], dtype=f32).T) for r in res.results])
    return out
```
